# Optimizing a Trainium2 kernel written in Bass

```python
import jax
import jax.numpy as jnp
from jax import lax
import numpy as np

D_MODEL = 2048
BATCH = 4
SEQ = 2048
DEPTH = 2

HEAD_DIM = 128
NSA_HEADS = D_MODEL // (2 * HEAD_DIM)
NSA_KV_HEADS = NSA_HEADS // 4
RET_HEADS = D_MODEL // (2 * HEAD_DIM)
NSA_W = NSA_HEADS * HEAD_DIM
NSA_KV_W = NSA_KV_HEADS * HEAD_DIM
RET_W = RET_HEADS * HEAD_DIM
CMP_BLOCK = 32
CMP_STRIDE = 16
SLC_BLOCK = 64
SLC_TOPK = 16
WINDOW = 512
NSA_Q_CHUNK = 64
WIN_Q_BLOCK = 128
RET_CHUNK = 128
N_MEM = 256
XA_HEADS = 4
XA_HEAD_DIM = D_MODEL // XA_HEADS
PEER_HEADS = 8
PEER_NKEYS = 128
PEER_N_EXPERTS = PEER_NKEYS * PEER_NKEYS
PEER_QUERY_DIM = 256
PEER_HALF = PEER_QUERY_DIM // 2
PEER_TOPK = 16
PEER_TOKEN_CHUNK = 128
ROPE_THETA = 10000.0
LN_EPS = 1e-5
GN_EPS = 1e-5
ALPHA = (2 * DEPTH) ** 0.25
BETA = (8 * DEPTH) ** -0.25
NEG_INF = -1e30
FORCE_SCORE = 1e9
IN_SIZES = (NSA_W, NSA_KV_W, NSA_KV_W, NSA_KV_W, NSA_KV_W, NSA_KV_W, NSA_KV_W, 3 * NSA_HEADS, RET_W, RET_W, RET_W, RET_W)
IN_COL_SCALE = (1.0, 1.0, BETA, 1.0, BETA, 1.0, BETA, 1.0, 1.0, 1.0, BETA, 1.0)
P_IN = sum(IN_SIZES)

kernel_name = 'hybrid_nsa_retention_peer_block'


def layer_norm(x, g, b):
    xf = x.astype(jnp.float32)
    mu = xf.mean(-1, keepdims=True)
    var = ((xf - mu) ** 2).mean(-1, keepdims=True)
    return ((xf - mu) * lax.rsqrt(var + LN_EPS) * g + b).astype(x.dtype)


def rope(x):
    S, Dh = x.shape[1], x.shape[-1]
    inv = 1.0 / (ROPE_THETA ** (jnp.arange(0, Dh, 2, dtype=jnp.float32) / Dh))
    ang = jnp.arange(S, dtype=jnp.float32)[:, None] * inv[None, :]
    cos = jnp.cos(ang)[:, None, :]
    sin = jnp.sin(ang)[:, None, :]
    xf = x.astype(jnp.float32)
    x1, x2 = xf[..., :Dh // 2], xf[..., Dh // 2:]
    return jnp.concatenate([x1 * cos - x2 * sin, x1 * sin + x2 * cos], -1).astype(x.dtype)


def nsa_attention(q, kc, vc, ks, vs, kw, vw, gate_logits, cmp_pos, cmp_w1, cmp_w2):
    B, S, H, Dh = q.shape
    Hkv = kc.shape[2]
    G = H // Hkv
    f32 = jnp.float32
    scale = Dh ** -0.5
    tpos = jnp.arange(S)

    n_cmp = (S - CMP_BLOCK) // CMP_STRIDE + 1
    blk_start = jnp.arange(n_cmp) * CMP_STRIDE
    gidx = blk_start[:, None] + jnp.arange(CMP_BLOCK)[None, :]

    def compress(t, i):
        tb = t[:, gidx] + cmp_pos[i][None, None, :, None, :]
        tb = tb.transpose(0, 1, 3, 2, 4).reshape(B, n_cmp, Hkv, CMP_BLOCK * Dh)
        return jax.nn.gelu(tb @ cmp_w1[i], approximate=False) @ cmp_w2[i]

    k_cmp = compress(kc, 0)
    v_cmp = compress(vc, 1)
    qg = q.reshape(B, S, Hkv, G, Dh)
    s_c = jnp.einsum('bsgrd,bngd->bgrsn', qg, k_cmp).astype(f32) * scale
    vis_c = (blk_start + CMP_BLOCK - 1)[None, :] <= tpos[:, None]
    p_c = jnp.where(vis_c, jax.nn.softmax(jnp.where(vis_c, s_c, NEG_INF), axis=-1), 0.0)
    o_c = jnp.einsum('bgrsn,bngd->bsgrd', p_c.astype(q.dtype), v_cmp).reshape(B, S, H, Dh)

    n_slc = S // SLC_BLOCK
    slc_start = jnp.arange(n_slc) * SLC_BLOCK
    overlap = ((blk_start[:, None] < (slc_start + SLC_BLOCK)[None, :]) &
               ((blk_start + CMP_BLOCK)[:, None] > slc_start[None, :])).astype(f32)
    imp = jnp.einsum('bgrsn,nj->bgsj', p_c, overlap)
    cur = tpos // SLC_BLOCK
    jb = jnp.arange(n_slc)
    forced = (jb[None, :] == 0) | (jb[None, :] == cur[:, None]) | (jb[None, :] == cur[:, None] - 1)
    score = jnp.where(forced, FORCE_SCORE, imp)
    score = jnp.where(slc_start[None, :] <= tpos[:, None], score, -1.0)
    n_top = min(SLC_TOPK, n_slc)
    _, sel = lax.top_k(score, n_top)

    q_r = rope(q)
    ks_r = rope(ks)
    kw_r = rope(kw)

    ks_blk = ks_r.reshape(B, n_slc, SLC_BLOCK, Hkv, Dh).transpose(0, 3, 1, 2, 4)
    vs_blk = vs.reshape(B, n_slc, SLC_BLOCK, Hkv, Dh).transpose(0, 3, 1, 2, 4)
    C = NSA_Q_CHUNK
    n_ch = S // C
    q_ch = q_r.reshape(B, n_ch, C, Hkv, G, Dh).transpose(1, 0, 3, 2, 4, 5)
    sel_ch = sel.reshape(B, Hkv, n_ch, C, n_top).transpose(2, 0, 1, 3, 4)
    pos_ch = tpos.reshape(n_ch, C)
    bi = jnp.arange(B)[:, None, None, None]
    gi = jnp.arange(Hkv)[None, :, None, None]

    def sel_chunk(args):
        qc, sc, pc = args
        kg = ks_blk[bi, gi, sc]
        vg = vs_blk[bi, gi, sc]
        kpos = sc[..., None] * SLC_BLOCK + jnp.arange(SLC_BLOCK)
        ok = (kpos <= pc[None, None, :, None, None])[:, :, :, None]
        s = jnp.einsum('bgcrd,bgcnld->bgcrnl', qc, kg).astype(f32) * scale
        s = jnp.where(ok, s, NEG_INF).reshape(B, Hkv, C, G, n_top * SLC_BLOCK)
        p = jax.nn.softmax(s, axis=-1).reshape(B, Hkv, C, G, n_top, SLC_BLOCK).astype(qc.dtype)
        return jnp.einsum('bgcrnl,bgcnld->bgcrd', p, vg)

    o_s = lax.map(sel_chunk, (q_ch, sel_ch, pos_ch))
    o_s = o_s.transpose(1, 0, 3, 2, 4, 5).reshape(B, S, H, Dh)

    WQ = WIN_Q_BLOCK
    n_qb = S // WQ
    span = WINDOW + WQ
    pad = ((0, 0), (WINDOW, 0), (0, 0), (0, 0))
    kw_pad = jnp.pad(kw_r, pad)
    vw_pad = jnp.pad(vw, pad)
    widx = jnp.arange(n_qb)[:, None] * WQ + jnp.arange(span)[None, :]
    kwin = kw_pad[:, widx]
    vwin = vw_pad[:, widx]
    kpos = widx - WINDOW
    qpos = tpos.reshape(n_qb, WQ)
    ok_w = ((kpos[:, None, :] <= qpos[:, :, None]) &
            (qpos[:, :, None] - kpos[:, None, :] < WINDOW) &
            (kpos[:, None, :] >= 0))
    qb = q_r.reshape(B, n_qb, WQ, Hkv, G, Dh)
    s_w = jnp.einsum('bqtgrd,bqkgd->bqgrtk', qb, kwin).astype(f32) * scale
    s_w = jnp.where(ok_w[None, :, None, None], s_w, NEG_INF)
    p_w = jax.nn.softmax(s_w, axis=-1).astype(q.dtype)
    o_w = jnp.einsum('bqgrtk,bqkgd->bqtgrd', p_w, vwin).reshape(B, S, H, Dh)

    g = jax.nn.sigmoid(gate_logits.astype(f32)).astype(q.dtype)
    return g[..., 0:1] * o_c + g[..., 1:2] * o_s + g[..., 2:3] * o_w


def retention(q, k, v, gate, gn_g, gn_b):
    B, S, H, Dh = q.shape
    f32 = jnp.float32
    q = rope(q).astype(f32)
    k = rope(k).astype(f32) * (Dh ** -0.5)
    v = v.astype(f32)
    log_g = jnp.log(1.0 - 2.0 ** (-5.0 - jnp.arange(H, dtype=f32)))
    C = RET_CHUNK
    n = S // C
    i = jnp.arange(C, dtype=f32)
    diff = i[:, None] - i[None, :]
    causal = diff >= 0
    dmask = jnp.where(causal[None], jnp.exp(jnp.where(causal, diff, 0.0)[None] * log_g[:, None, None]), 0.0)
    xi = jnp.exp((i[None, :] + 1.0) * log_g[:, None])
    zeta = jnp.exp((C - 1.0 - i[None, :]) * log_g[:, None])
    cdec = jnp.exp(C * log_g)

    def to_ch(t):
        return t.reshape(B, n, C, H, t.shape[-1]).transpose(1, 0, 3, 2, 4)

    def step(R, inp):
        qi, ki, vi = inp
        inner = jnp.einsum('bhid,bhjd->bhij', qi, ki) * dmask
        o = jnp.einsum('bhij,bhjv->bhiv', inner, vi) + jnp.einsum('bhid,bhdv->bhiv', qi, R) * xi[None, :, :, None]
        R = R * cdec[None, :, None, None] + jnp.einsum('bhjd,bhjv->bhdv', ki * zeta[None, :, :, None], vi)
        return R, o

    R0 = jnp.zeros((B, H, Dh, v.shape[-1]), f32)
    _, ys = lax.scan(step, R0, (to_ch(q), to_ch(k), to_ch(v)))
    y = ys.transpose(1, 0, 3, 2, 4).reshape(B, S, H, -1)
    mu = y.mean(-1, keepdims=True)
    var = ((y - mu) ** 2).mean(-1, keepdims=True)
    y = ((y - mu) * lax.rsqrt(var + GN_EPS)).reshape(B, S, -1) * gn_g + gn_b
    return (jax.nn.silu(gate.astype(f32)) * y).astype(gate.dtype)


def hybrid_mixer(x, w_in, b_gate, cmp_pos, cmp_w1, cmp_w2, gn_g, gn_b, w_out):
    B, S, _ = x.shape
    proj = x @ w_in
    offs = np.cumsum(IN_SIZES)[:-1].tolist()
    (q, kc, vc, ks, vs, kw, vw, gl, rq, rk, rv, rg) = jnp.split(proj, offs, axis=-1)

    def heads(t, h):
        return t.reshape(B, S, h, HEAD_DIM)

    o_nsa = nsa_attention(heads(q, NSA_HEADS), heads(kc, NSA_KV_HEADS), heads(vc, NSA_KV_HEADS),
                          heads(ks, NSA_KV_HEADS), heads(vs, NSA_KV_HEADS),
                          heads(kw, NSA_KV_HEADS), heads(vw, NSA_KV_HEADS),
                          (gl + b_gate).reshape(B, S, NSA_HEADS, 3), cmp_pos, cmp_w1, cmp_w2)
    o_ret = retention(heads(rq, RET_HEADS), heads(rk, RET_HEADS), heads(rv, RET_HEADS), rg, gn_g, gn_b)
    return jnp.concatenate([o_nsa.reshape(B, S, NSA_W), o_ret], axis=-1) @ w_out


def memory_cross_attention(x, mem, wq, wk, wv, wo):
    B, S, D = x.shape
    M = mem.shape[1]
    q = (x @ wq).reshape(B, S, XA_HEADS, XA_HEAD_DIM)
    k = (mem @ wk).reshape(B, M, XA_HEADS, XA_HEAD_DIM)
    v = (mem @ wv).reshape(B, M, XA_HEADS, XA_HEAD_DIM)
    s = jnp.einsum('bshd,bmhd->bhsm', q, k).astype(jnp.float32) * (XA_HEAD_DIM ** -0.5)
    p = jax.nn.softmax(s, axis=-1).astype(x.dtype)
    o = jnp.einsum('bhsm,bmhd->bshd', p, v).reshape(B, S, D)
    return o @ wo


def peer_ffn(x, w_q, sub_keys, u_tab, v_tab):
    B, S, D = x.shape
    T = B * S
    xt = x.reshape(T, D)
    q = (xt @ w_q).reshape(T, PEER_HEADS, 2, PEER_HALF)
    s = jnp.einsum('thpk,hpnk->thpn', q, sub_keys).astype(jnp.float32)
    s1, i1 = lax.top_k(s[:, :, 0], PEER_TOPK)
    s2, i2 = lax.top_k(s[:, :, 1], PEER_TOPK)
    cand = (s1[..., :, None] + s2[..., None, :]).reshape(T, PEER_HEADS, PEER_TOPK * PEER_TOPK)
    top, pos = lax.top_k(cand, PEER_TOPK)
    e = (jnp.take_along_axis(i1, pos // PEER_TOPK, axis=-1) * PEER_NKEYS +
         jnp.take_along_axis(i2, pos % PEER_TOPK, axis=-1))
    g = jax.nn.softmax(top, axis=-1).astype(x.dtype)
    C = PEER_TOKEN_CHUNK
    n_ch = T // C

    def chunk(args):
        xc, ec, gc = args
        h = jax.nn.gelu(jnp.einsum('td,thkd->thk', xc, u_tab[ec]), approximate=False)
        return jnp.einsum('thk,thkd->td', gc * h, v_tab[ec])

    out = lax.map(chunk, (xt.reshape(n_ch, C, D), e.reshape(n_ch, C, PEER_HEADS, PEER_TOPK),
                          g.reshape(n_ch, C, PEER_HEADS, PEER_TOPK)))
    return out.reshape(B, S, D)


def setup_inputs(seed: int = 0) -> dict:
    key = jax.random.key(seed)
    k = jax.random.split(key, 24)
    L, D, f32 = DEPTH, D_MODEL, jnp.float32

    def nrm(kk, shape, scale):
        return jax.random.normal(kk, shape, f32) * scale

    col_scale = jnp.concatenate([jnp.full((n,), s, f32) for n, s in zip(IN_SIZES, IN_COL_SCALE)])
    return {
        'x': nrm(k[0], (BATCH, SEQ, D), 1.0),
        'mem': nrm(k[1], (BATCH, N_MEM, D), 1.0),
        'w_in': nrm(k[2], (L, D, P_IN), D ** -0.5) * col_scale,
        'b_gate': nrm(k[3], (L, 3 * NSA_HEADS), 0.01),
        'cmp_pos': nrm(k[4], (L, 2, CMP_BLOCK, HEAD_DIM), 0.02),
        'cmp_w1': nrm(k[5], (L, 2, CMP_BLOCK * HEAD_DIM, HEAD_DIM), (CMP_BLOCK * HEAD_DIM) ** -0.5),
        'cmp_w2': nrm(k[6], (L, 2, HEAD_DIM, HEAD_DIM), HEAD_DIM ** -0.5),
        'ret_gn_g': 1.0 + nrm(k[7], (L, RET_W), 0.02),
        'ret_gn_b': nrm(k[8], (L, RET_W), 0.02),
        'w_mix_out': nrm(k[9], (L, D, D), BETA * D ** -0.5),
        'ln1_g': 1.0 + nrm(k[10], (L, D), 0.02),
        'ln1_b': nrm(k[11], (L, D), 0.02),
        'xa_wq': nrm(k[12], (L, D, D), D ** -0.5),
        'xa_wk': nrm(k[13], (L, D, D), D ** -0.5),
        'xa_wv': nrm(k[14], (L, D, D), BETA * D ** -0.5),
        'xa_wo': nrm(k[15], (L, D, D), BETA * D ** -0.5),
        'ln2_g': 1.0 + nrm(k[16], (L, D), 0.02),
        'ln2_b': nrm(k[17], (L, D), 0.02),
        'peer_wq': nrm(k[18], (L, D, PEER_HEADS * PEER_QUERY_DIM), D ** -0.5),
        'peer_sub_keys': nrm(k[19], (L, PEER_HEADS, 2, PEER_NKEYS, PEER_HALF), PEER_HALF ** -0.5),
        'peer_u': nrm(k[20], (L, PEER_N_EXPERTS, D), D ** -0.5),
        'peer_v': nrm(k[21], (L, PEER_N_EXPERTS, D), BETA * PEER_HEADS ** -0.5),
        'ln3_g': 1.0 + nrm(k[22], (L, D), 0.02),
        'ln3_b': nrm(k[23], (L, D), 0.02),
    }


def reference(x, mem, w_in, b_gate, cmp_pos, cmp_w1, cmp_w2, ret_gn_g, ret_gn_b, w_mix_out,
              ln1_g, ln1_b, xa_wq, xa_wk, xa_wv, xa_wo, ln2_g, ln2_b,
              peer_wq, peer_sub_keys, peer_u, peer_v, ln3_g, ln3_b):
    for l in range(DEPTH):
        h = hybrid_mixer(x, w_in[l], b_gate[l], cmp_pos[l], cmp_w1[l], cmp_w2[l],
                         ret_gn_g[l], ret_gn_b[l], w_mix_out[l])
        x = layer_norm(ALPHA * x + h, ln1_g[l], ln1_b[l])
        h = memory_cross_attention(x, mem, xa_wq[l], xa_wk[l], xa_wv[l], xa_wo[l])
        x = layer_norm(ALPHA * x + h, ln2_g[l], ln2_b[l])
        h = peer_ffn(x, peer_wq[l], peer_sub_keys[l], peer_u[l], peer_v[l])
        x = layer_norm(ALPHA * x + h, ln3_g[l], ln3_b[l])
    return x
```

```python
import numpy as np
import concourse.bass as bass
import concourse.mybir as mybir
from concourse.bass_utils import run_bass_kernel_spmd

F32 = mybir.dt.float32
BF16 = mybir.dt.bfloat16
I32 = mybir.dt.int32
U32 = mybir.dt.uint32
AF = mybir.ActivationFunctionType
ALU = mybir.AluOpType
AX = mybir.AxisListType

S = 2048
D = 2048
DEPTH = 2
ALPHA = (2 * DEPTH) ** 0.25
BETA = (8 * DEPTH) ** -0.25
EPS = 1e-5
SC128 = 128.0 ** -0.5
SCXA = 512.0 ** -0.5


class Prog:
    SEM_LIMIT = 20000

    def __init__(self):
        self.nc = bass.Bass("TRN2", target_bir_lowering=False)
        nc = self.nc
        self.eng = {'pe': nc.tensor, 'dve': nc.vector, 'act': nc.scalar,
                    'pool': nc.gpsimd, 'sp': nc.sync}
        self.sem = {}
        self.cnt = {}
        self.nsem = 0
        for e in self.eng:
            self._new_sem(e)
        self.seen = {e: {} for e in self.eng}
        self.bufs = {}
        self.dslots = {}
        self.dnext = {}
        for q, k in (('sp', 8), ('pool', 8), ('act', 2)):
            self.dslots[q] = [[nc.alloc_semaphore(name=f"d_{q}_{i}"), 0] for i in range(k)]
            self.dnext[q] = 0
        self.ninst = 0
        self.rr = {}

    def _new_sem(self, e):
        self.sem[e] = self.nc.alloc_semaphore(name=f"p_{e}_{self.nsem}")
        self.nsem += 1
        self.cnt[e] = 0

    def sb(self, name, shape, dt):
        return self.nc.alloc_sbuf_tensor(name, list(shape), dt)

    def ps(self, name, shape, dt=F32):
        return self.nc.alloc_psum_tensor(name, list(shape), dt)

    def dram(self, name, shape, dt, kind):
        return self.nc.dram_tensor(name, list(shape), dt, kind=kind).ap()

    def _buf(self, t):
        b = self.bufs.get(t)
        if b is None:
            b = self.bufs[t] = {'w': None, 'r': {}}
        return b

    def _deps(self, e, reads, writes):
        deps = []
        for t in reads:
            b = self._buf(t)
            if b['w'] is not None:
                deps.append(b['w'])
        for t in writes:
            b = self._buf(t)
            if b['w'] is not None:
                deps.append(b['w'])
            for ev in b['r'].values():
                if ev[2] == e:
                    continue
                deps.append(ev)
        waits = {}
        for (s, v, src) in deps:
            if src == e and e == 'pe':
                continue
            key = id(s)
            if self.seen[e].get(key, 0) >= v:
                continue
            if key not in waits or waits[key][1] < v:
                waits[key] = (s, v)
        for key, (s, v) in waits.items():
            self.eng[e].wait_ge(s, v)
            self.seen[e][key] = v

    def _record(self, ev, reads, writes, rkey):
        for t in reads:
            self._buf(t)['r'][rkey] = ev
        for t in writes:
            b = self._buf(t)
            b['w'] = ev
            b['r'] = {}

    @staticmethod
    def _bank(t):
        if not isinstance(t, str):
            return None
        if t.startswith('pb') and t[2:].isdigit():
            return int(t[2:])
        if t[:2] in ('ri', 'ro', 'ru') and t[2:].isdigit():
            return int(t[2:])
        if t[:2] == 'kt' and t[2:].isdigit():
            return 4
        return None

    def op(self, e, fn, reads=(), writes=()):
        locks = set()
        for t in list(reads) + list(writes):
            b = self._bank(t)
            if b is not None:
                locks.add(f'L{b}')
        if locks:
            reads = list(reads) + sorted(locks)
            writes = list(writes) + sorted(locks)
        self._deps(e, reads, writes)
        if self.cnt[e] >= self.SEM_LIMIT:
            self._new_sem(e)
        inst = fn(self.eng[e])
        self.cnt[e] += 1
        inst.then_inc(self.sem[e], 1)
        ev = (self.sem[e], self.cnt[e], e)
        self._record(ev, reads, writes, e)
        self.ninst += 1
        return inst

    def dma(self, q, out, in_, reads=(), writes=(), fn=None):
        self._deps(q, reads, writes)
        slots = self.dslots[q]
        k = self.dnext[q]
        self.dnext[q] = (k + 1) % len(slots)
        s, c = slots[k]
        if c > 0 and self.seen[q].get(id(s), 0) < 16 * c:
            self.eng[q].wait_ge(s, 16 * c)
            self.seen[q][id(s)] = 16 * c
        if fn is None:
            inst = self.eng[q].dma_start(out=out, in_=in_)
        else:
            inst = fn(self.eng[q])
        inst.then_inc(s, 16)
        slots[k][1] = c + 1
        ev = (s, 16 * (c + 1), 'dma')
        self._record(ev, reads, writes, ('dma', q, k))
        self.ninst += 1
        return inst

    def barrier(self):
        for e in self.eng:
            for e2 in self.eng:
                if e2 != e and self.cnt[e2] > 0:
                    s, v = self.sem[e2], self.cnt[e2]
                    if self.seen[e].get(id(s), 0) < v:
                        self.eng[e].wait_ge(s, v)
                        self.seen[e][id(s)] = v
            for q, slots in self.dslots.items():
                for s, c in slots:
                    if c > 0 and self.seen[e].get(id(s), 0) < 16 * c:
                        self.eng[e].wait_ge(s, 16 * c)
                        self.seen[e][id(s)] = 16 * c
        self.bufs = {}

    def finish(self):
        for q, slots in self.dslots.items():
            for s, c in slots:
                if c > 0:
                    self.eng['sp'].wait_ge(s, 16 * c)

    def alt(self, key, n):
        v = self.rr.get(key, 0)
        self.rr[key] = (v + 1) % n
        return v


class Ctx:
    pass


class Region:
    def __init__(self, t, n):
        self.t, self.n, self.cur = t, n, 0

    def reset(self):
        self.cur = 0

    def take(self, shape, dt):
        ne = 1
        for v in shape[1:]:
            ne *= v
        nb = ne * (2 if dt == F32 or dt == I32 or dt == U32 else 1)
        nb = (nb + 1) // 2 * 2
        assert self.cur + nb <= self.n, (self.cur, nb, self.n)
        ap = self.t[:, self.cur:self.cur + nb]
        self.cur += nb
        if dt != BF16:
            ap = ap.bitcast(dt)
        if len(shape) == 3:
            ap = ap.rearrange("p (a b) -> p a b", a=shape[1])
        elif len(shape) == 4:
            ap = ap.rearrange("p (a b c) -> p a b c", a=shape[1], b=shape[2])
        return ap


class Blob:
    def __init__(self, shapes):
        self.shapes = dict(shapes)
        self.off = {}
        o = 0
        for k, shp in self.shapes.items():
            self.off[k] = o
            o += int(np.prod(shp))
        self.n = o

    def declare(self, P):
        self.ap = P.dram("blob", [self.n], F32, "ExternalInput")

    def __getitem__(self, k):
        shp = self.shapes[k]
        n = int(np.prod(shp))
        v = self.ap[self.off[k]:self.off[k] + n]
        if len(shp) == 2:
            return v.rearrange("(a b) -> a b", a=shp[0])
        if len(shp) == 3:
            return v.rearrange("(a b c) -> a b c", a=shp[0], b=shp[1])
        if len(shp) == 4:
            return v.rearrange("(a b c d) -> a b c d", a=shp[0], b=shp[1], c=shp[2])
        return v

    def pack(self, d):
        out = np.empty((self.n,), np.float32)
        for k, shp in self.shapes.items():
            a = np.asarray(d[k], np.float32)
            assert tuple(a.shape) == tuple(shp), (k, a.shape, shp)
            out[self.off[k]:self.off[k] + a.size] = a.reshape(-1)
        return out


M_SHAPES = {
    'x': (S, D), 'wfm': (16, 128, 16, 128), 'wta': (1, 128, 16, 268), 'wtr': (4, 128, 16, 256),
    'bg': (1, 12), 'posT': (128, 2, 32), 'w1': (2, 128, 32, 128), 'w2': (128, 2, 128),
    'gng': (1, 512), 'gnb': (1, 512), 'cos2': (128, S), 'sin2': (128, S), 'perm': (128, 128),
    'dmT': (128, 4, 128), 'xib': (128, 4, 128), 'zc': (128, 8), 'visT': (128, S), 'ovl': (128, 32),
    'tri': (128, 2, 128), 'tk': (128, 3, 16, 32),
}


def common_setup(P, C):
    C.pb = [P.ps(f"pb{i}", [128, 512], F32) for i in range(8)]
    C.ident = P.sb("ident", [128, 128], BF16)
    iof = P.sb("iof", [128, 128], F32)
    iop = P.sb("iop", [128, 1], F32)
    C.iof = iof
    P.op('pool', lambda e: e.iota(iof[:], pattern=[[1, 128]], base=0, channel_multiplier=0,
                                  allow_small_or_imprecise_dtypes=True), writes=['iof'])
    P.op('pool', lambda e: e.iota(iop[:], pattern=[[0, 1]], base=0, channel_multiplier=1,
                                  allow_small_or_imprecise_dtypes=True), writes=['iop'])
    P.op('dve', lambda e: e.tensor_scalar(out=C.ident[:], in0=iof[:], scalar1=iop[:, 0:1], scalar2=None,
                                          op0=ALU.is_equal), reads=['iof', 'iop'], writes=['ident'])
    C.epsc = P.sb("epsc", [128, 1], F32)
    P.op('pool', lambda e: e.memset(C.epsc[:], EPS), writes=['epsc'])


def transpose_rows(P, C, src, src_tag, dstT, dst_tag, col0, nk=16, bank=2):
    for kb in range(0, nk, 8):
        pbb = C.pb[bank][:].bitcast(BF16)
        cnt = min(8, nk - kb)
        for kk in range(cnt):
            P.op('pe', lambda e, kk=kk: e.transpose(pbb[:, kk * 128:(kk + 1) * 128],
                                                   src[:, (kb + kk) * 128:(kb + kk + 1) * 128], C.ident[:]),
                 reads=[src_tag, 'ident'], writes=[f'pb{bank}'])
        P.op('act', lambda e: e.activation(out=dstT[:, kb:kb + cnt, col0:col0 + 128],
                                           in_=pbb[:, 0:cnt * 128].rearrange("p (k s) -> p k s", k=cnt),
                                           func=AF.Copy),
             reads=[f'pb{bank}'], writes=[dst_tag])


def lin_fm(P, C, w_dram, ccs, actT, act_tag, ns, evac, wbufs, nk=16, banks=(0, 1)):
    step = min(512, ns)
    for cc in ccs:
        j = P.alt('wfm', len(wbufs))
        wb, wtag = wbufs[j], f'wfm{j}'
        P.dma('pool', wb[:, 0:nk, :], w_dram[cc], writes=[wtag])
        for t0 in range(0, ns, step):
            b = banks[P.alt('fmbank', len(banks))]
            for k in range(nk):
                P.op('pe', lambda e, k=k: e.matmul(C.pb[b][:, 0:step], lhsT=wb[:, k, :],
                                                  rhs=actT[:, k, t0:t0 + step],
                                                  start=(k == 0), stop=(k == nk - 1)),
                     reads=[wtag, act_tag], writes=[f'pb{b}'])
            evac(cc, t0, step, C.pb[b], f'pb{b}')


def lin_tm(P, C, w_dram, cgs, cw, actT, act_tag, tiles, evac, wbufs, nk=16, banks=(0, 1)):
    for cg in cgs:
        j = P.alt('wtm', len(wbufs))
        wb, wtag = wbufs[j], f'wtm{j}'
        P.dma('pool', wb[:, 0:nk, 0:cw], w_dram[cg], writes=[wtag])
        for ti in tiles:
            b = banks[P.alt('tmbank', len(banks))]
            for k in range(nk):
                P.op('pe', lambda e, k=k: e.matmul(C.pb[b][:, 0:cw], lhsT=actT[:, k, ti * 128:(ti + 1) * 128],
                                                  rhs=wb[:, k, 0:cw],
                                                  start=(k == 0), stop=(k == nk - 1)),
                     reads=[wtag, act_tag], writes=[f'pb{b}'])
            evac(cg, ti, C.pb[b], f'pb{b}')


def build_M(stop=99):
    P = Prog()
    C = Ctx()
    MB = Blob(M_SHAPES)
    MB.declare(P)
    x_d = MB["x"]
    wfm_d = MB["wfm"]
    wta_d = MB["wta"]
    wtr_d = MB["wtr"]
    bg_d = MB["bg"]
    posT_d = MB["posT"]
    w1_d = MB["w1"]
    w2_d = MB["w2"]
    gng_d = MB["gng"]
    gnb_d = MB["gnb"]
    cos_d = MB["cos2"]
    sin_d = MB["sin2"]
    perm_d = MB["perm"]
    dm_d = MB["dmT"]
    xi_d = MB["xib"]
    zc_d = MB["zc"]
    vis_d = MB["visT"]
    ovl_d = MB["ovl"]
    tri_d = MB["tri"]
    tk_d = MB["tk"]
    om_d = P.dram("omix", [S, 1024], F32, "ExternalOutput")

    common_setup(P, C)
    pb = C.pb
    xT = P.sb("xT", [128, 16, S], BF16)
    U = P.sb("U", [128, 32768], BF16)
    cos2 = P.sb("cos2s", [128, S], BF16)
    sin2 = P.sb("sin2s", [128, S], BF16)
    wfb = [P.sb(f"wfb{i}", [128, 16, 128], BF16) for i in range(2)]
    wtb = [P.sb("wtb0", [128, 16, 268], BF16)]
    Vt = P.sb("V", [128, 10752], BF16)
    V = Region(Vt, 10752)
    XR = Region(xT[:].rearrange("p k s -> p (k s)"), 16 * S)
    xb = [V.take([128, D], BF16) for i in range(2)]
    permb = P.sb("permb", [128, 128], BF16)
    P.dma('pool', cos2[:], cos_d, writes=['cos2'])
    P.dma('pool', sin2[:], sin_d, writes=['sin2'])
    P.dma('pool', permb[:], perm_d, writes=['permb'])

    for i in range(16):
        j = i % 2
        P.dma('pool', xb[j][:], x_d[i * 128:(i + 1) * 128, :], writes=[f'xb{j}'])
        transpose_rows(P, C, xb[j], f'xb{j}', xT, 'xT', i * 128, bank=2 + j)

    if stop <= 1:
        P.finish()
        return P
    tb = [P.sb(f"tb{i}", [128, 512], BF16) for i in range(2)]
    ta = [P.sb(f"ta{i}", [128, 512], F32) for i in range(2)]
    ta2 = [P.sb(f"tc{i}", [128, 512], F32) for i in range(2)]

    def rope_evac(psum, ptag, t0, dest_ap, dest_tag, extra=None):
        j = P.alt('rope', 2)
        P.op('act', lambda e: e.activation(out=tb[j][:], in_=psum[:, 0:512], func=AF.Copy),
             reads=[ptag], writes=[f'tb{j}'])
        if extra is not None:
            extra()
        P.op('pe', lambda e: e.matmul(pb[3][:, 0:512], lhsT=permb[:], rhs=tb[j][:], start=True, stop=True),
             reads=['permb', f'tb{j}'], writes=['pb3'])
        P.op('dve', lambda e: e.tensor_tensor(out=ta[j][:], in0=psum[:, 0:512], in1=cos2[:, t0:t0 + 512], op=ALU.mult),
             reads=[ptag, 'cos2'], writes=[f'ta{j}'])
        P.op('dve', lambda e: e.tensor_tensor(out=ta2[j][:], in0=pb[3][:, 0:512], in1=sin2[:, t0:t0 + 512], op=ALU.mult),
             reads=['pb3', 'sin2'], writes=[f'tc{j}'])
        if len(dest_ap.shape) == 3:
            i0 = ta[j][:].rearrange("p (i s) -> p i s", i=4)
            i1 = ta2[j][:].rearrange("p (i s) -> p i s", i=4)
        else:
            i0, i1 = ta[j][:], ta2[j][:]
        P.op('pool', lambda e: e.tensor_tensor(out=dest_ap, in0=i0, in1=i1, op=ALU.add),
             reads=[f'ta{j}', f'tc{j}'], writes=[dest_tag])

    rqT = U[:, 0:8192].rearrange("p (n h s) -> p n h s", n=16, h=4)
    rkT = U[:, 8192:16384].rearrange("p (n h s) -> p n h s", n=16, h=4)
    rv = U[:, 16384:24576].rearrange("p (n f) -> p n f", n=16)
    rgs = U[:, 24576:32768].rearrange("p (n f) -> p n f", n=16)

    def evac_ret_fm(cc, t0, nt, psum, ptag):
        h = (cc - 8) % 4
        dst, dtag = (rqT, 'rqT') if cc < 12 else (rkT, 'rkT')
        sg = t0 // 512
        rope_evac(psum, ptag, t0, dst[:, 4 * sg:4 * sg + 4, h, :], dtag)

    lin_fm(P, C, wfm_d, range(8, 16), xT, 'xT', S, evac_ret_fm, wfb)

    def evac_ret_tm(cg, ti, psum, ptag):
        cs = slice(256 * (cg % 2), 256 * (cg % 2) + 256)
        if cg < 2:
            P.op('act', lambda e: e.activation(out=rv[:, ti, cs], in_=psum[:, 0:256], func=AF.Copy),
                 reads=[ptag], writes=['rv'])
        else:
            P.op('act', lambda e: e.activation(out=rgs[:, ti, cs], in_=psum[:, 0:256], func=AF.Silu),
                 reads=[ptag], writes=['rgs'])

    if stop <= 2:
        P.finish()
        return P
    lin_tm(P, C, wtr_d, range(4), 256, xT, 'xT', range(16), evac_ret_tm, wtb)
    if stop <= 3:
        P.finish()
        return P

    P.barrier()
    V.reset()
    dmT = V.take([128, 4, 128], F32)
    xib = V.take([128, 4, 128], F32)
    zc = V.take([128, 8], F32)
    gng = V.take([128, 512], F32)
    gnb = V.take([128, 512], F32)
    P.dma('sp', dmT[:], dm_d, writes=['dmT'])
    P.dma('sp', xib[:], xi_d, writes=['xib'])
    P.dma('sp', zc[:], zc_d, writes=['zc'])
    P.dma('sp', gng[:], gng_d.partition_broadcast(128), writes=['gng'])
    P.dma('sp', gnb[:], gnb_d.partition_broadcast(128), writes=['gnb'])
    Rf = V.take([128, 4, 128], F32)
    Rb = V.take([128, 4, 128], BF16)
    inT = V.take([128, 4, 128], BF16)
    kz = V.take([128, 4, 128], BF16)
    qxi = V.take([128, 4, 128], BF16)
    st = V.take([128, 4, 6], F32)
    mv = V.take([128, 4, 2], F32)
    rs = V.take([128, 4], F32)
    yn = V.take([128, 4, 128], F32)
    orow = [V.take([128, 512], F32) for i in range(2)]
    pb4b = pb[4][:].bitcast(BF16)
    for n in range(16):
        oj = n % 2
        for h in range(4):
            kT = rkT[:, n, h, :]
            qT = rqT[:, n, h, :]
            v = rv[:, n, 128 * h:128 * (h + 1)]
            hs = slice(128 * h, 128 * (h + 1))
            P.op('pe', lambda e: e.matmul(pb[h][:, 0:128], lhsT=kT, rhs=qT, start=True, stop=True),
                 reads=['rkT', 'rqT'], writes=[f'ri{h}'])
            P.op('dve', lambda e: e.tensor_tensor(out=inT[:, h, :], in0=pb[h][:, 0:128], in1=dmT[:, h, :], op=ALU.mult),
                 reads=[f'ri{h}', 'dmT'], writes=[f'inT{h}'])
            if n < 15:
                P.op('pe', lambda e: e.transpose(pb4b[:, hs], kT, C.ident[:]),
                     reads=['rkT', 'ident'], writes=[f'kt{h}'])
                P.op('act', lambda e: e.activation(out=kz[:, h, :], in_=pb4b[:, hs], func=AF.Copy, scale=zc[:, h:h + 1]),
                     reads=[f'kt{h}', 'zc'], writes=[f'kz{h}'])
            if n > 0:
                P.op('pool', lambda e: e.tensor_tensor(out=qxi[:, h, :], in0=qT, in1=xib[:, h, :], op=ALU.mult),
                     reads=['rqT', 'xib'], writes=[f'qxi{h}'])
            P.op('pe', lambda e: e.matmul(pb[h][:, 128:256], lhsT=inT[:, h, :], rhs=v, start=True, stop=(n == 0)),
                 reads=[f'inT{h}', 'rv'], writes=[f'ro{h}'])
            if n > 0:
                P.op('pe', lambda e: e.matmul(pb[h][:, 128:256], lhsT=qxi[:, h, :], rhs=Rb[:, h, :], start=False, stop=True),
                     reads=[f'qxi{h}', f'Rb{h}'], writes=[f'ro{h}'])
            if n < 15:
                P.op('pe', lambda e: e.matmul(pb[h][:, 256:384], lhsT=kz[:, h, :], rhs=v, start=True, stop=True),
                     reads=[f'kz{h}', 'rv'], writes=[f'ru{h}'])
                if n == 0:
                    P.op('dve', lambda e: e.tensor_copy(out=Rf[:, h, :], in_=pb[h][:, 256:384]),
                         reads=[f'ru{h}'], writes=[f'Rf{h}'])
                else:
                    P.op('dve', lambda e: e.scalar_tensor_tensor(out=Rf[:, h, :], in0=Rf[:, h, :], scalar=zc[:, 4 + h:5 + h],
                                                                 in1=pb[h][:, 256:384], op0=ALU.mult, op1=ALU.add),
                         reads=[f'ru{h}', f'Rf{h}', 'zc'], writes=[f'Rf{h}'])
                P.op('act', lambda e: e.activation(out=Rb[:, h, :], in_=Rf[:, h, :], func=AF.Copy),
                     reads=[f'Rf{h}'], writes=[f'Rb{h}'])
            P.op('dve', lambda e: e.bn_stats(out=st[:, h, :], in_=pb[h][:, 128:256]), reads=[f'ro{h}'], writes=[f'st{h}'])
            P.op('dve', lambda e: e.bn_aggr(out=mv[:, h, :], in_=st[:, h, :]), reads=[f'st{h}'], writes=[f'mv{h}'])
            P.op('act', lambda e: e.activation(out=rs[:, h:h + 1], in_=mv[:, h, 1:2], func=AF.Sqrt, bias=C.epsc[:, 0:1]),
                 reads=[f'mv{h}', 'epsc'], writes=[f'rs{h}'])
            P.op('dve', lambda e: e.reciprocal(out=rs[:, h:h + 1], in_=rs[:, h:h + 1]), reads=[f'rs{h}'], writes=[f'rs{h}'])
            P.op('dve', lambda e: e.tensor_scalar(out=yn[:, h, :], in0=pb[h][:, 128:256], scalar1=mv[:, h, 0:1],
                                                  scalar2=rs[:, h:h + 1], op0=ALU.subtract, op1=ALU.mult),
                 reads=[f'ro{h}', f'mv{h}', f'rs{h}'], writes=[f'yn{h}'])
            P.op('pool', lambda e: e.tensor_tensor(out=yn[:, h, :], in0=yn[:, h, :], in1=gng[:, hs], op=ALU.mult),
                 reads=[f'yn{h}', 'gng'], writes=[f'yn{h}'])
            P.op('pool', lambda e: e.tensor_tensor(out=yn[:, h, :], in0=yn[:, h, :], in1=gnb[:, hs], op=ALU.add),
                 reads=[f'yn{h}', 'gnb'], writes=[f'yn{h}'])
            P.op('pool', lambda e: e.tensor_tensor(out=orow[oj][:, hs], in0=yn[:, h, :], in1=rgs[:, n, hs], op=ALU.mult),
                 reads=[f'yn{h}', 'rgs'], writes=[f'orow{oj}'])
        P.dma('sp', om_d[n * 128:(n + 1) * 128, 512:1024], orow[oj][:], reads=[f'orow{oj}'], writes=[f'omr{n}'])

    if stop <= 4:
        P.finish()
        return P
    P.barrier()
    V.reset()

    qro = U[:, 0:8192].rearrange("p (i r s) -> p i r s", i=16, r=4)
    qun = U[:, 8192:16384].rearrange("p (i r s) -> p i r s", i=16, r=4)
    kcT = U[:, 16384:18432]
    vcT = U[:, 18432:20480]
    ksT = U[:, 20480:22528]
    kwT = U[:, 22528:24576]
    vsa = U[:, 24576:26656].rearrange("p (i f) -> p i f", i=16)
    vwa = U[:, 26656:28736].rearrange("p (i f) -> p i f", i=16)
    P.op('pool', lambda e: e.memset(vsa[:, :, 128:130], 1.0), writes=['vsa'])
    P.op('pool', lambda e: e.memset(vwa[:, :, 128:130], 1.0), writes=['vwa'])

    def evac_nsa_fm(cc, t0, nt, psum, ptag):
        sg = t0 // 512
        if cc < 4:
            def extra():
                P.op('act', lambda e: e.activation(out=qun[:, 4 * sg:4 * sg + 4, cc, :],
                                                   in_=psum[:, 0:512].rearrange("p (i s) -> p i s", i=4), func=AF.Copy),
                     reads=[ptag], writes=['qun'])
            rope_evac(psum, ptag, t0, qro[:, 4 * sg:4 * sg + 4, cc, :], 'qro', extra)
        elif cc == 4:
            P.op('act', lambda e: e.activation(out=kcT[:, t0:t0 + 512], in_=psum[:, 0:512], func=AF.Copy), reads=[ptag], writes=['kcT'])
        elif cc == 5:
            P.op('act', lambda e: e.activation(out=vcT[:, t0:t0 + 512], in_=psum[:, 0:512], func=AF.Copy), reads=[ptag], writes=['vcT'])
        elif cc == 6:
            rope_evac(psum, ptag, t0, ksT[:, t0:t0 + 512], 'ksT')
        else:
            rope_evac(psum, ptag, t0, kwT[:, t0:t0 + 512], 'kwT')

    lin_fm(P, C, wfm_d, range(0, 8), xT, 'xT', S, evac_nsa_fm, wfb)

    bgs = V.take([128, 12], F32)
    gsb = V.take([128, 16, 12], F32)
    gtmp = V.take([128, 12], F32)
    P.dma('sp', bgs[:], bg_d.partition_broadcast(128), writes=['bgs'])

    def evac_nsa_tm(cg, ti, psum, ptag):
        P.op('act', lambda e: e.activation(out=vsa[:, ti, 0:128], in_=psum[:, 0:128], func=AF.Copy), reads=[ptag], writes=['vsa'])
        P.op('act', lambda e: e.activation(out=vwa[:, ti, 0:128], in_=psum[:, 128:256], func=AF.Copy), reads=[ptag], writes=['vwa'])
        P.op('dve', lambda e: e.tensor_tensor(out=gtmp[:], in0=psum[:, 256:268], in1=bgs[:], op=ALU.add), reads=[ptag, 'bgs'], writes=['gtmp'])
        P.op('act', lambda e: e.activation(out=gsb[:, ti, :], in_=gtmp[:], func=AF.Sigmoid), reads=['gtmp'], writes=['gsb'])

    lin_tm(P, C, wta_d, range(1), 268, xT, 'xT', range(16), evac_nsa_tm, wtb)

    if stop <= 5:
        P.finish()
        return P
    P.barrier()
    visT = XR.take([128, S], BF16)
    tri = XR.take([128, 2, 128], BF16)
    tk = XR.take([128, 3, 16, 32], F32)
    cat = XR.take([128, 161], BF16)
    w1b = XR.take([128, 2, 32, 128], BF16)
    w2b = XR.take([128, 2, 128], BF16)
    posb = XR.take([128, 2, 32], BF16)
    P.dma('pool', visT[:], vis_d, writes=['visT'])
    P.dma('pool', tri[:], tri_d, writes=['tri'])
    P.dma('sp', tk[:], tk_d, writes=['tk'])
    P.op('pool', lambda e: e.memset(cat[:], 0.0), writes=['cat'])
    P.op('pool', lambda e: e.memset(cat[:, 128:129], 1.0), writes=['cat'])
    P.dma('pool', cat[:, 129:161], ovl_d, writes=['cat'])
    for j in range(2):
        P.dma('pool', w1b[:, j, :, :], w1_d[j], writes=['w1b'])
    P.dma('pool', w2b[:], w2_d, writes=['w2b'])
    P.dma('pool', posb[:], posT_d, writes=['posb'])

    cst = XR.take([128, 2], F32)
    hid = XR.take([128, 2, 128], BF16)
    kcmpT = XR.take([128, 128], BF16)
    P.op('pool', lambda e: e.memset(hid[:], 0.0), writes=['hid'])
    for j in range(2):
        src, stag = (kcT, 'kcT') if j == 0 else (vcT, 'vcT')
        for l in range(32):
            P.op('pe', lambda e: e.matmul(pb[7][:, 0:1], lhsT=w1b[:, j, l, :], rhs=posb[:, j, l:l + 1],
                                          start=(l == 0), stop=(l == 31)), reads=['w1b', 'posb'], writes=['pb7'])
        P.op('dve', lambda e: e.tensor_copy(out=cst[:, j:j + 1], in_=pb[7][:, 0:1]), reads=['pb7'], writes=['cst'])
        for l in range(32):
            P.op('pe', lambda e: e.matmul(pb[6][:, 0:127], lhsT=w1b[:, j, l, :], rhs=src[:, l:l + 2017:16],
                                          start=(l == 0), stop=(l == 31)), reads=['w1b', stag], writes=['pb6'])
        P.op('act', lambda e: e.activation(out=hid[:, j, 0:127], in_=pb[6][:, 0:127], func=AF.Gelu, bias=cst[:, j:j + 1]),
             reads=['pb6', 'cst'], writes=['hid'])
    P.op('pe', lambda e: e.matmul(pb[7][:, 128:256], lhsT=w2b[:, 0, :], rhs=hid[:, 0, :], start=True, stop=True),
         reads=['w2b', 'hid'], writes=['pb7'])
    P.op('act', lambda e: e.activation(out=kcmpT[:], in_=pb[7][:, 128:256], func=AF.Copy), reads=['pb7'], writes=['kcmpT'])
    P.op('pe', lambda e: e.matmul(pb[6][:, 128:256], lhsT=hid[:, 1, :], rhs=w2b[:, 1, :], start=True, stop=True),
         reads=['w2b', 'hid'], writes=['pb6'])
    P.op('act', lambda e: e.activation(out=cat[:, 0:128], in_=pb[6][:, 128:256], func=AF.Copy), reads=['pb6'], writes=['cat'])

    if stop <= 6:
        P.finish()
        return P
    eb = [XR.take([128, 4, 128], BF16) for i in range(2)]
    pt = [XR.take([128, 4, 128], BF16) for i in range(2)]
    onsa = [XR.take([128, 4, 128], F32) for i in range(2)]
    rz = XR.take([128, 16], F32)
    imps = XR.take([128, 32], F32)
    score = XR.take([128, 32], F32)
    work = XR.take([128, 32], F32)
    mx8 = XR.take([128, 16], F32)
    sel = XR.take([128, 32], F32)
    selx = XR.take([128, 32, 64], BF16)
    selxf = selx.rearrange("p a b -> p (a b)")
    mT = XR.take([128, 16, 128], BF16)
    pb2b = pb[2][:].bitcast(BF16)
    rzi = [0]

    def acc_ap(banks, r):
        return pb[banks[r // 2]][:, (r % 2) * 256:(r % 2) * 256 + 161]

    accs = XR.take([128, 4, 132], F32)

    def finish_branch(i, banks, tags, gidx, first, from_sb=False):
        oj = i % 2
        for r in range(4):
            a = accs[:, r, :] if from_sb else acc_ap(banks, r)
            if from_sb:
                tags = ('accs', 'accs')
            k = rzi[0] % 8
            rzi[0] += 1
            zt = f'rz{k}'
            P.op('dve', lambda e: e.tensor_scalar(out=rz[:, k:k + 1], in0=a[:, 128:129], scalar1=1e-30, scalar2=None, op0=ALU.max),
                 reads=[tags[r // 2]], writes=[zt])
            P.op('dve', lambda e: e.reciprocal(out=rz[:, k:k + 1], in_=rz[:, k:k + 1]), reads=[zt], writes=[zt])
            if gidx == 0:
                if r == 0:
                    P.op('dve', lambda e: e.tensor_scalar(out=imps[:], in0=a[:, 129:161], scalar1=rz[:, k:k + 1], scalar2=None, op0=ALU.mult),
                         reads=[tags[r // 2], zt], writes=['imps'])
                else:
                    P.op('dve', lambda e: e.scalar_tensor_tensor(out=imps[:], in0=a[:, 129:161], scalar=rz[:, k:k + 1], in1=imps[:],
                                                                 op0=ALU.mult, op1=ALU.add),
                         reads=[tags[r // 2], zt, 'imps'], writes=['imps'])
            P.op('dve', lambda e: e.tensor_tensor(out=rz[:, 8 + k:9 + k], in0=rz[:, k:k + 1], in1=gsb[:, i, 3 * r + gidx:3 * r + gidx + 1], op=ALU.mult),
                 reads=[zt, 'gsb'], writes=[zt + 'c'])
            if first:
                P.op('act', lambda e: e.activation(out=onsa[oj][:, r, :], in_=a[:, 0:128], func=AF.Copy, scale=rz[:, 8 + k:9 + k]),
                     reads=[tags[r // 2], zt + 'c'], writes=[f'onsa{oj}'])
            else:
                P.op('dve', lambda e: e.scalar_tensor_tensor(out=onsa[oj][:, r, :], in0=a[:, 0:128], scalar=rz[:, 8 + k:9 + k],
                                                             in1=onsa[oj][:, r, :], op0=ALU.mult, op1=ALU.add),
                     reads=[tags[r // 2], zt + 'c', f'onsa{oj}'], writes=[f'onsa{oj}'])

    def attn_chunks(i, chunks, kT, ktag, va, vtag, maskfn, banks, tags):
        qi = qro[:, i, :, :].rearrange("p r s -> p (r s)")
        for ci, c in enumerate(chunks):
            sb_ = P.alt('scb', 2)
            j = P.alt('ebj', 2)
            P.op('pe', lambda e: e.matmul(pb[sb_][:, 0:512], lhsT=kT[:, c * 128:(c + 1) * 128], rhs=qi, start=True, stop=True),
                 reads=[ktag, 'qro'], writes=[f'pb{sb_}'])
            P.op('act', lambda e: e.activation(out=eb[j][:].rearrange("p r s -> p (r s)"), in_=pb[sb_][:, 0:512], func=AF.Exp, scale=SC128),
                 reads=[f'pb{sb_}'], writes=[f'eb{j}'])
            m = maskfn(c)
            if m is not None:
                map_, mtag = m
                P.op('dve', lambda e: e.tensor_tensor(out=pt[j][:], in0=eb[j][:], in1=map_.unsqueeze(1).to_broadcast([128, 4, 128]), op=ALU.mult),
                     reads=[f'eb{j}', mtag], writes=[f'pt{j}'])
                lt, ltag = pt[j], f'pt{j}'
            else:
                lt, ltag = eb[j], f'eb{j}'
            for r in range(4):
                P.op('pe', lambda e: e.matmul(acc_ap(banks, r)[:, 0:129], lhsT=lt[:, r, :], rhs=va[:, c, 0:129],
                                              start=True, stop=True),
                     reads=[ltag, vtag], writes=[tags[r // 2]])
            for hb in range(2):
                pv = pb[banks[hb]][:, 0:512].rearrange("p (a b) -> p a b", a=2)[:, :, 0:129]
                if ci == 0:
                    P.op('dve', lambda e: e.tensor_copy(out=accs[:, 2 * hb:2 * hb + 2, 0:129], in_=pv),
                         reads=[tags[hb]], writes=['accs'])
                else:
                    P.op('dve', lambda e: e.tensor_tensor(out=accs[:, 2 * hb:2 * hb + 2, 0:129], in0=pv,
                                                          in1=accs[:, 2 * hb:2 * hb + 2, 0:129], op=ALU.add),
                         reads=[tags[hb], 'accs'], writes=['accs'])

    for i in range(16):
        oj = i % 2
        sb_ = P.alt('scb', 2)
        j = P.alt('ebj', 2)
        P.op('pe', lambda e: e.matmul(pb[sb_][:, 0:512], lhsT=kcmpT[:], rhs=qun[:, i, :, :].rearrange("p r s -> p (r s)"),
                                      start=True, stop=True), reads=['kcmpT', 'qun'], writes=[f'pb{sb_}'])
        P.op('act', lambda e: e.activation(out=eb[j][:].rearrange("p r s -> p (r s)"), in_=pb[sb_][:, 0:512], func=AF.Exp, scale=SC128),
             reads=[f'pb{sb_}'], writes=[f'eb{j}'])
        P.op('dve', lambda e: e.tensor_tensor(out=pt[j][:], in0=eb[j][:], in1=visT[:, i * 128:(i + 1) * 128].unsqueeze(1).to_broadcast([128, 4, 128]), op=ALU.mult),
             reads=[f'eb{j}', 'visT'], writes=[f'pt{j}'])
        cb, ctags = (3, 4), ('pb3', 'pb4')
        for r in range(4):
            P.op('pe', lambda e: e.matmul(acc_ap(cb, r), lhsT=pt[j][:, r, :], rhs=cat[:, 0:161], start=True, stop=True),
                 reads=[f'pt{j}', 'cat'], writes=[ctags[r // 2]])
        finish_branch(i, cb, ctags, 0, True)
        P.op('dve', lambda e: e.tensor_tensor(out=score[:], in0=imps[:], in1=tk[:, 0, i, :], op=ALU.mult), reads=['imps', 'tk'], writes=['score'])
        P.op('dve', lambda e: e.tensor_tensor(out=score[:], in0=score[:], in1=tk[:, 1, i, :], op=ALU.add), reads=['score', 'tk'], writes=['score'])
        P.op('dve', lambda e: e.max(out=mx8[:, 0:8], in_=score[:]), reads=['score'], writes=['mx8'])
        P.op('dve', lambda e: e.match_replace(out=work[:], in_to_replace=mx8[:, 0:8], in_values=score[:], imm_value=-1e30),
             reads=['score', 'mx8'], writes=['work'])
        P.op('dve', lambda e: e.max(out=mx8[:, 8:16], in_=work[:]), reads=['work'], writes=['mx8b'])
        P.op('dve', lambda e: e.tensor_scalar(out=sel[:], in0=score[:], scalar1=mx8[:, 15:16], scalar2=None, op0=ALU.is_ge),
             reads=['score', 'mx8b'], writes=['sel'])
        P.op('dve', lambda e: e.tensor_tensor(out=sel[:], in0=sel[:], in1=tk[:, 2, i, :], op=ALU.mult), reads=['sel', 'tk'], writes=['sel'])
        nb = 2 * (i + 1)
        P.op('pool', lambda e: e.tensor_copy(out=selx[:, 0:nb, :], in_=sel[:, 0:nb].unsqueeze(2).to_broadcast([128, nb, 64])),
             reads=['sel'], writes=['selx'])
        for c0 in range(0, i + 1, 8):
            cn = min(8, i + 1 - c0)
            for cc_ in range(cn):
                P.op('pe', lambda e: e.transpose(pb2b[:, cc_ * 128:(cc_ + 1) * 128], selxf[:, (c0 + cc_) * 128:(c0 + cc_ + 1) * 128], C.ident[:]),
                     reads=['selx', 'ident'], writes=['pb2'])
            P.op('act', lambda e: e.activation(out=mT[:, c0:c0 + cn, :], in_=pb2b[:, 0:cn * 128].rearrange("p (c s) -> p c s", c=cn), func=AF.Copy),
                 reads=['pb2'], writes=['mT'])
        P.op('pool', lambda e: e.tensor_tensor(out=mT[:, i, :], in0=mT[:, i, :], in1=tri[:, 0, :], op=ALU.mult), reads=['mT', 'tri'], writes=['mT'])
        sbk, stags = (5, 6), ('pb5', 'pb6')
        attn_chunks(i, list(range(i + 1)), ksT, 'ksT', vsa, 'vsa', lambda c: (mT[:, c, :], 'mT'), sbk, stags)
        finish_branch(i, sbk, stags, 1, False, True)
        def wmask(c):
            if c == i:
                return (tri[:, 0, :], 'tri')
            if c == i - 4:
                return (tri[:, 1, :], 'tri')
            return None
        attn_chunks(i, list(range(max(0, i - 4), i + 1)), kwT, 'kwT', vwa, 'vwa', wmask, cb, ctags)
        finish_branch(i, cb, ctags, 2, False, True)
        P.dma('sp', om_d[i * 128:(i + 1) * 128, 0:512], onsa[oj][:].rearrange("p r s -> p (r s)"), reads=[f'onsa{oj}'], writes=[f'omn{i}'])

    P.finish()
    return P


NT = 1024
T_SHAPES = {
    'om': (NT, 2048), 'xin': (NT, 2048), 'mem': (256, 2048),
    'wout': (8, 128, 16, 256), 'wq': (16, 128, 16, 128), 'wk': (16, 128, 16, 128),
    'wv': (8, 128, 16, 256), 'wo': (8, 128, 16, 256), 'pwq': (16, 128, 16, 128),
    'ln': (6, 1, 2048), 'skT': (128, 16, 128),
}


def build_T(stop=99, nsub=2):
    P = Prog()
    C = Ctx()
    TB = Blob(T_SHAPES)
    TB.declare(P)
    om_d, xin_d, mem_d = TB['om'], TB['xin'], TB['mem']
    ln_d, sk_d = TB['ln'], TB['skT']
    tab_d = P.dram("tab", [32768, 2048], F32, "ExternalInput")
    xo_d = P.dram("xout", [NT, 2048], F32, "ExternalOutput")
    common_setup(P, C)
    pb = C.pb
    xres = P.sb("xres", [128, 4, 2048], F32)
    aT = P.sb("aT", [128, 16, 512], BF16)
    qT = P.sb("qT", [128, 16, 512], BF16)
    memT = P.sb("memT", [128, 16, 256], BF16)
    KT = P.sb("KT", [128, 16, 256], BF16)
    Vv = P.sb("Vv", [128, 2, 2048], BF16)
    wfb = [P.sb(f"wfb{i}", [128, 16, 128], BF16) for i in range(2)]
    wtb = [P.sb(f"wtb{i}", [128, 16, 256], BF16) for i in range(2)]
    xb = P.sb("xb", [128, 2048], BF16)
    lng = P.sb("lng", [128, 2048], F32)
    lnb = P.sb("lnb", [128, 2048], F32)
    ug = [P.sb(f"ug{i}", [128, 2048], F32) for i in range(2)]
    acc = P.sb("acc", [128, 2048], F32)
    junk = P.sb("junk", [128, 2048], BF16)
    skT = P.sb("skTs", [128, 16, 128], BF16)
    pT = [P.sb(f"pT{i}", [128, 512], BF16) for i in range(2)]
    rden = P.sb("rden", [128, 512], F32)
    onesb = P.sb("onesb", [128, 128], BF16)
    lst = P.sb("lst", [128, 4, 6], F32)
    lmv = P.sb("lmv", [128, 2], F32)
    lrs = P.sb("lrs", [128, 1], F32)
    sc = P.sb("sc", [128, 2, 128], F32)
    work = P.sb("work", [128, 256], F32)
    mx = P.sb("mx", [128, 2, 16], F32)
    ix = P.sb("ix", [128, 2, 16], U32)
    ixf = P.sb("ixf", [128, 2, 16], F32)
    cand = P.sb("cand", [128, 16, 16], F32)
    cmx = P.sb("cmx", [128, 16], F32)
    cpos = P.sb("cpos", [128, 16], U32)
    cab = P.sb("cab", [128, 2, 16], U32)
    cabf = P.sb("cabf", [128, 2, 16], F32)
    eq = P.sb("eq", [128, 16, 16], F32)
    isel = P.sb("isel", [128, 2, 16], F32)
    ef = P.sb("ef", [128, 16], F32)
    eidx = P.sb("eidx", [128, 4, 128], I32)
    eidx2 = P.sb("eidx2", [128, 4, 128], I32)
    ef2 = P.sb("ef2", [128, 16], F32)
    gw = P.sb("gw", [128, 4, 128], F32)
    gsm = P.sb("gsm", [128, 4], F32)
    ge = P.sb("ge", [128, 16], F32)
    hcol = P.sb("hcol", [128, 128], F32)
    coef = P.sb("coef", [128, 128], F32)
    P.op('pool', lambda e: e.memset(onesb[:], 1.0), writes=['onesb'])
    P.dma('pool', skT[:], sk_d, writes=['skT'])

    def layer_norm(ti, which):
        xt = xres[:, ti, :]
        tg = f'xres{ti}'
        for c4 in range(4):
            P.op('dve', lambda e: e.bn_stats(out=lst[:, c4, :], in_=xt[:, c4 * 512:(c4 + 1) * 512]), reads=[tg], writes=['lst'])
        P.op('dve', lambda e: e.bn_aggr(out=lmv[:], in_=lst[:].rearrange("p a b -> p (a b)")), reads=['lst'], writes=['lmv'])
        P.op('act', lambda e: e.activation(out=lrs[:], in_=lmv[:, 1:2], func=AF.Sqrt, bias=C.epsc[:, 0:1]), reads=['lmv', 'epsc'], writes=['lrs'])
        P.op('dve', lambda e: e.reciprocal(out=lrs[:], in_=lrs[:]), reads=['lrs'], writes=['lrs'])
        P.op('dve', lambda e: e.tensor_scalar(out=xt, in0=xt, scalar1=lmv[:, 0:1], scalar2=lrs[:, 0:1], op0=ALU.subtract, op1=ALU.mult),
             reads=[tg, 'lmv', 'lrs'], writes=[tg])
        P.op('pool', lambda e: e.tensor_tensor(out=xt, in0=xt, in1=lng[:], op=ALU.mult), reads=[tg, 'lng'], writes=[tg])
        P.op('pool', lambda e: e.tensor_tensor(out=xt, in0=xt, in1=lnb[:], op=ALU.add), reads=[tg, 'lnb'], writes=[tg])

    def load_ln(which):
        P.dma('sp', lng[:], ln_d[2 * which].partition_broadcast(128), writes=['lng'])
        P.dma('sp', lnb[:], ln_d[2 * which + 1].partition_broadcast(128), writes=['lnb'])

    def resid_evac(cg, ti, psum, ptag):
        cs = slice(256 * cg, 256 * cg + 256)
        P.op('dve', lambda e: e.scalar_tensor_tensor(out=xres[:, ti, cs], in0=xres[:, ti, cs], scalar=float(ALPHA), in1=psum[:, 0:256],
                                                     op0=ALU.mult, op1=ALU.add), reads=[ptag, f'xres{ti}'], writes=[f'xres{ti}'])

    def xres_to_aT():
        for ti in range(4):
            P.op('act', lambda e: e.activation(out=xb[:], in_=xres[:, ti, :], func=AF.Copy), reads=[f'xres{ti}'], writes=['xb'])
            transpose_rows(P, C, xb, 'xb', aT, 'aT', ti * 128, bank=2 + ti % 2)

    for mt in range(2):
        P.dma('pool', xb[:], mem_d[mt * 128:(mt + 1) * 128, :], writes=['xb'])
        transpose_rows(P, C, xb, 'xb', memT, 'memT', mt * 128, bank=2 + mt)

    def evac_k(cc, t0, nt, psum, ptag):
        P.op('act', lambda e: e.activation(out=KT[:, cc, :], in_=psum[:, 0:256], func=AF.Copy), reads=[ptag], writes=['KT'])
    lin_fm(P, C, TB['wk'], range(16), memT, 'memT', 256, evac_k, wfb)

    def evac_v(cg, ti, psum, ptag):
        P.op('act', lambda e: e.activation(out=Vv[:, ti, 256 * cg:256 * cg + 256], in_=psum[:, 0:256], func=AF.Copy), reads=[ptag], writes=['Vv'])
    lin_tm(P, C, TB['wv'], range(8), 256, memT, 'memT', range(2), evac_v, wtb)

    def writeback(sh):
        for ti in range(4):
            P.dma('sp', xo_d[sh * 512 + ti * 128:sh * 512 + (ti + 1) * 128, :], xres[:, ti, :], reads=[f'xres{ti}'], writes=[f'xo{sh}_{ti}'])

    for sh in range(nsub):
        r0 = sh * 512
        for ti in range(4):
            P.dma('sp', xres[:, ti, :], xin_d[r0 + ti * 128:r0 + (ti + 1) * 128, :], writes=[f'xres{ti}'])
        for ti in range(4):
            P.dma('pool', xb[:], om_d[r0 + ti * 128:r0 + (ti + 1) * 128, :], writes=['xb'])
            transpose_rows(P, C, xb, 'xb', aT, 'aT', ti * 128, bank=2 + ti % 2)
        lin_tm(P, C, TB['wout'], range(8), 256, aT, 'aT', range(4), resid_evac, wtb)
        load_ln(0)
        for ti in range(4):
            layer_norm(ti, 0)
        if stop <= 1:
            writeback(sh)
            continue
        xres_to_aT()

        def evac_q(cc, t0, nt, psum, ptag):
            P.op('act', lambda e: e.activation(out=qT[:, cc, :], in_=psum[:, 0:512], func=AF.Copy), reads=[ptag], writes=['qT'])
        lin_fm(P, C, TB['wq'], range(16), aT, 'aT', 512, evac_q, wfb)
        for h in range(4):
            for mt in range(2):
                bk = 4 + mt
                for kk in range(4):
                    P.op('pe', lambda e: e.matmul(pb[bk][:, 0:512], lhsT=KT[:, 4 * h + kk, mt * 128:(mt + 1) * 128], rhs=qT[:, 4 * h + kk, :],
                                                  start=(kk == 0), stop=(kk == 3)), reads=['KT', 'qT'], writes=[f'pb{bk}'])
                P.op('act', lambda e: e.activation(out=pT[mt][:], in_=pb[bk][:, 0:512], func=AF.Exp, scale=SCXA), reads=[f'pb{bk}'], writes=[f'pT{mt}'])
            for mt in range(2):
                P.op('pe', lambda e: e.matmul(pb[6][:, 0:512], lhsT=onesb[:], rhs=pT[mt][:], start=(mt == 0), stop=(mt == 1)),
                     reads=['onesb', f'pT{mt}'], writes=['pb6'])
            P.op('dve', lambda e: e.reciprocal(out=rden[:], in_=pb[6][:, 0:512]), reads=['pb6'], writes=['rden'])
            for dv in range(4):
                bk = 7 if dv % 2 == 0 else 3
                for mt in range(2):
                    P.op('pe', lambda e: e.matmul(pb[bk][:, 0:512], lhsT=Vv[:, mt, 512 * h + 128 * dv:512 * h + 128 * dv + 128], rhs=pT[mt][:],
                                                  start=(mt == 0), stop=(mt == 1)), reads=['Vv', f'pT{mt}'], writes=[f'pb{bk}'])
                P.op('dve', lambda e: e.tensor_tensor(out=aT[:, 4 * h + dv, :], in0=pb[bk][:, 0:512], in1=rden[:], op=ALU.mult),
                     reads=[f'pb{bk}', 'rden'], writes=['aT'])
        lin_tm(P, C, TB['wo'], range(8), 256, aT, 'aT', range(4), resid_evac, wtb)
        load_ln(1)
        for ti in range(4):
            layer_norm(ti, 1)
        if stop <= 2:
            writeback(sh)
            continue
        xres_to_aT()
        lin_fm(P, C, TB['pwq'], range(16), aT, 'aT', 512, evac_q, wfb)
        iota16 = C.iof[:, 0:16]
        for ti in range(4):
            for h in range(8):
                bk = 4 + (h % 2)
                for p in range(2):
                    P.op('pe', lambda e: e.matmul(pb[bk][:, 128 * p:128 * p + 128], lhsT=qT[:, 2 * h + p, ti * 128:(ti + 1) * 128], rhs=skT[:, 2 * h + p, :],
                                                  start=True, stop=True), reads=['qT', 'skT'], writes=[f'pb{bk}'])
                P.op('act', lambda e: e.activation(out=sc[:].rearrange("p a b -> p (a b)"), in_=pb[bk][:, 0:256], func=AF.Copy), reads=[f'pb{bk}'], writes=['sc'])
                for p in range(2):
                    P.op('dve', lambda e: e.max(out=mx[:, p, 0:8], in_=sc[:, p, :]), reads=['sc'], writes=['mx'])
                    P.op('dve', lambda e: e.max_index(out=ix[:, p, 0:8], in_max=mx[:, p, 0:8], in_values=sc[:, p, :]), reads=['sc', 'mx'], writes=['ix'])
                    P.op('dve', lambda e: e.match_replace(out=work[:, 0:128], in_to_replace=mx[:, p, 0:8], in_values=sc[:, p, :], imm_value=-1e30),
                         reads=['sc', 'mx'], writes=['work'])
                    P.op('dve', lambda e: e.max(out=mx[:, p, 8:16], in_=work[:, 0:128]), reads=['work'], writes=['mx'])
                    P.op('dve', lambda e: e.max_index(out=ix[:, p, 8:16], in_max=mx[:, p, 8:16], in_values=work[:, 0:128]), reads=['work', 'mx'], writes=['ix'])
                P.op('dve', lambda e: e.tensor_tensor(out=cand[:], in0=mx[:, 0, :].unsqueeze(2).to_broadcast([128, 16, 16]),
                                                      in1=mx[:, 1, :].unsqueeze(1).to_broadcast([128, 16, 16]), op=ALU.add), reads=['mx'], writes=['cand'])
                cf = cand[:].rearrange("p a b -> p (a b)")
                P.op('dve', lambda e: e.max(out=cmx[:, 0:8], in_=cf), reads=['cand'], writes=['cmx'])
                P.op('dve', lambda e: e.max_index(out=cpos[:, 0:8], in_max=cmx[:, 0:8], in_values=cf), reads=['cand', 'cmx'], writes=['cpos'])
                P.op('dve', lambda e: e.match_replace(out=work[:], in_to_replace=cmx[:, 0:8], in_values=cf, imm_value=-1e30), reads=['cand', 'cmx'], writes=['work'])
                P.op('dve', lambda e: e.max(out=cmx[:, 8:16], in_=work[:]), reads=['work'], writes=['cmx'])
                P.op('dve', lambda e: e.max_index(out=cpos[:, 8:16], in_max=cmx[:, 8:16], in_values=work[:]), reads=['work', 'cmx'], writes=['cpos'])
                P.op('dve', lambda e: e.tensor_single_scalar(out=cab[:, 0, :], in_=cpos[:], scalar=4, op=ALU.logical_shift_right), reads=['cpos'], writes=['cab'])
                P.op('dve', lambda e: e.tensor_single_scalar(out=cab[:, 1, :], in_=cpos[:], scalar=15, op=ALU.bitwise_and), reads=['cpos'], writes=['cab'])
                P.op('dve', lambda e: e.tensor_copy(out=cabf[:], in_=cab[:]), reads=['cab'], writes=['cabf'])
                P.op('dve', lambda e: e.tensor_copy(out=ixf[:], in_=ix[:]), reads=['ix'], writes=['ixf'])
                for p in range(2):
                    P.op('dve', lambda e: e.tensor_tensor(out=eq[:], in0=cabf[:, p, :].unsqueeze(2).to_broadcast([128, 16, 16]),
                                                          in1=iota16.unsqueeze(1).to_broadcast([128, 16, 16]), op=ALU.is_equal), reads=['cabf', 'iof'], writes=['eq'])
                    P.op('dve', lambda e: e.tensor_tensor(out=eq[:], in0=eq[:], in1=ixf[:, p, :].unsqueeze(1).to_broadcast([128, 16, 16]), op=ALU.mult),
                         reads=['eq', 'ixf'], writes=['eq'])
                    P.op('dve', lambda e: e.tensor_reduce(out=isel[:, p, :], in_=eq[:], axis=AX.X, op=ALU.add), reads=['eq'], writes=['isel'])
                P.op('dve', lambda e: e.scalar_tensor_tensor(out=ef[:], in0=isel[:, 0, :], scalar=128.0, in1=isel[:, 1, :], op0=ALU.mult, op1=ALU.add),
                     reads=['isel'], writes=['ef'])
                P.op('dve', lambda e: e.tensor_copy(out=eidx[:, ti, 16 * h:16 * h + 16], in_=ef[:]), reads=['ef'], writes=['eidx'])
                P.op('dve', lambda e: e.tensor_scalar(out=ef2[:], in0=ef[:], scalar1=16384.0, scalar2=None, op0=ALU.add), reads=['ef'], writes=['ef2'])
                P.op('dve', lambda e: e.tensor_copy(out=eidx2[:, ti, 16 * h:16 * h + 16], in_=ef2[:]), reads=['ef2'], writes=['eidx2'])
                P.op('dve', lambda e: e.tensor_scalar(out=gsm[:, 0:1], in0=cmx[:, 0:1], scalar1=-1.0, scalar2=None, op0=ALU.mult), reads=['cmx'], writes=['gsm'])
                P.op('act', lambda e: e.activation(out=ge[:], in_=cmx[:], func=AF.Exp, bias=gsm[:, 0:1], accum_out=gsm[:, 1:2]), reads=['cmx', 'gsm'], writes=['ge', 'gsm1'])
                P.op('dve', lambda e: e.reciprocal(out=gsm[:, 2:3], in_=gsm[:, 1:2]), reads=['gsm1'], writes=['gsm2'])
                P.op('dve', lambda e: e.tensor_scalar(out=gw[:, ti, 16 * h:16 * h + 16], in0=ge[:], scalar1=gsm[:, 2:3], scalar2=None, op0=ALU.mult),
                     reads=['ge', 'gsm2'], writes=['gw'])
        load_ln(2)
        for ti in range(4):
            tg = f'xres{ti}'
            P.op('pool', lambda e: e.memset(hcol[:], 0.0), writes=['hcol'])
            for sl in range(128):
                j = P.alt('ug', 2)
                P.dma('pool', None, None, reads=['eidx'], writes=[f'ug{j}'],
                      fn=lambda e: e.indirect_dma_start(out=ug[j][:], out_offset=None, in_=tab_d,
                                                        in_offset=bass.IndirectOffsetOnAxis(ap=eidx[:, ti, sl:sl + 1], axis=0)))
                P.op('dve', lambda e: e.scalar_tensor_tensor(out=junk[:], in0=xres[:, ti, :], scalar=1.0, in1=ug[j][:], op0=ALU.mult, op1=ALU.mult,
                                                             accum_out=hcol[:, sl:sl + 1]), reads=[tg, f'ug{j}'], writes=['junk', 'hcol'])
            P.op('act', lambda e: e.activation(out=coef[:], in_=hcol[:], func=AF.Gelu), reads=['hcol'], writes=['coef'])
            P.op('dve', lambda e: e.tensor_tensor(out=coef[:], in0=coef[:], in1=gw[:, ti, :], op=ALU.mult), reads=['coef', 'gw'], writes=['coef'])
            for sl in range(128):
                j = P.alt('ug', 2)
                P.dma('pool', None, None, reads=['eidx2'], writes=[f'ug{j}'],
                      fn=lambda e: e.indirect_dma_start(out=ug[j][:], out_offset=None, in_=tab_d,
                                                        in_offset=bass.IndirectOffsetOnAxis(ap=eidx2[:, ti, sl:sl + 1], axis=0)))
                if sl == 0:
                    P.op('dve', lambda e: e.tensor_scalar(out=acc[:], in0=ug[j][:], scalar1=coef[:, 0:1], scalar2=None, op0=ALU.mult),
                         reads=[f'ug{j}', 'coef'], writes=['acc'])
                else:
                    P.op('dve', lambda e: e.scalar_tensor_tensor(out=acc[:], in0=ug[j][:], scalar=coef[:, sl:sl + 1], in1=acc[:], op0=ALU.mult, op1=ALU.add),
                         reads=[f'ug{j}', 'coef', 'acc'], writes=['acc'])
            P.op('dve', lambda e: e.scalar_tensor_tensor(out=xres[:, ti, :], in0=xres[:, ti, :], scalar=float(ALPHA), in1=acc[:], op0=ALU.mult, op1=ALU.add),
                 reads=[tg, 'acc'], writes=[tg])
            layer_norm(ti, 2)
        writeback(sh)
    P.finish()
    return P

def arr_fm(w):
    n = w.shape[1] // 128
    return np.ascontiguousarray(w.reshape(16, 128, n, 128).transpose(2, 1, 0, 3))


def arr_tm(w, cw):
    n = w.shape[1] // cw
    ko = w.shape[0] // 128
    return np.ascontiguousarray(w.reshape(ko, 128, n, cw).transpose(2, 1, 0, 3))


IN_SIZES = (1024, 256, 256, 256, 256, 256, 256, 24, 1024, 1024, 1024, 1024)
IN_OFFS = np.concatenate([[0], np.cumsum(IN_SIZES)]).astype(int)


def m_consts(g):
    f32 = np.float32
    c = {}
    inv = 1.0 / (10000.0 ** (np.arange(0, 128, 2, dtype=f32) / f32(128)))
    ang = np.arange(S, dtype=f32)[:, None] * inv[None, :]
    cos = np.cos(ang).astype(f32).T
    sin = np.sin(ang).astype(f32).T
    c['cos2'] = np.ascontiguousarray(np.concatenate([cos, cos], 0))
    c['sin2'] = np.ascontiguousarray(np.concatenate([-sin, sin], 0))
    perm = np.zeros((128, 128), f32)
    for m in range(128):
        perm[(m + 64) % 128, m] = 1.0
    c['perm'] = perm
    H = 8
    log_g = np.log(1.0 - 2.0 ** (-5.0 - np.arange(H, dtype=f32))).astype(f32)
    i = np.arange(128, dtype=f32)
    diff = i[:, None] - i[None, :]
    causal = diff >= 0
    dmask = np.where(causal[None], np.exp(np.where(causal, diff, 0.0)[None] * log_g[:, None, None]), 0.0).astype(f32)
    xi = np.exp((i[None, :] + 1.0) * log_g[:, None]).astype(f32)
    zeta = np.exp((128 - 1.0 - i[None, :]) * log_g[:, None]).astype(f32)
    cdec = np.exp(128 * log_g).astype(f32)
    hs = slice(4 * g, 4 * g + 4)
    c['dmT'] = np.ascontiguousarray((dmask[hs] * f32(SC128)).transpose(2, 0, 1))
    c['xib'] = np.ascontiguousarray(np.broadcast_to(xi[hs][None], (128, 4, 128))).astype(f32)
    zc = np.zeros((128, 8), f32)
    zc[:, 0:4] = (zeta[hs] * f32(SC128)).T
    zc[:, 4:8] = cdec[hs][None, :]
    c['zc'] = zc
    n = np.arange(128)
    s = np.arange(S)
    vis = ((16 * n[:, None] + 31) <= s[None, :]) & (n[:, None] < 127)
    c['visT'] = vis.astype(f32)
    blk = n * 16
    slc = np.arange(32) * 64
    ovl = ((blk[:, None] < (slc + 64)[None, :]) & ((blk + 32)[:, None] > slc[None, :]) & (n[:, None] < 127))
    c['ovl'] = ovl.astype(f32)
    kl = np.arange(128)
    tri = (kl[:, None] <= kl[None, :]).astype(f32)
    atri = (kl[:, None] > kl[None, :]).astype(f32)
    c['tri'] = np.ascontiguousarray(np.stack([tri, atri], 1))
    cur = s // 64
    jb = np.arange(32)
    forced = (jb[None, :] == 0) | (jb[None, :] == cur[:, None]) | (jb[None, :] == cur[:, None] - 1)
    caus = slc[None, :] <= s[:, None]
    A = ((~forced) & caus).astype(f32)
    Bm = (1e9 * (forced & caus).astype(f32) - (~caus).astype(f32)).astype(f32)
    tk = np.stack([A, Bm, caus.astype(f32)], 0).reshape(3, 16, 128, 32).transpose(2, 0, 1, 3)
    c['tk'] = np.ascontiguousarray(tk)
    return c


def m_inputs(inp, l, xb, g, consts):
    w = inp['w_in'][l]
    o = IN_OFFS

    def sec(k, a, b):
        return w[:, o[k] + a:o[k] + b]
    fm = np.concatenate([sec(0, 512 * g, 512 * g + 512), sec(1, 128 * g, 128 * g + 128), sec(2, 128 * g, 128 * g + 128),
                         sec(3, 128 * g, 128 * g + 128), sec(5, 128 * g, 128 * g + 128),
                         sec(8, 512 * g, 512 * g + 512), sec(9, 512 * g, 512 * g + 512)], 1)
    ta = np.concatenate([sec(4, 128 * g, 128 * g + 128), sec(6, 128 * g, 128 * g + 128), sec(7, 12 * g, 12 * g + 12)], 1)
    tr = np.concatenate([sec(10, 512 * g, 512 * g + 512), sec(11, 512 * g, 512 * g + 512)], 1)
    d = dict(consts)
    d['x'] = np.ascontiguousarray(xb)
    d['wfm'] = arr_fm(fm)
    d['wta'] = arr_tm(ta, 268)
    d['wtr'] = arr_tm(tr, 256)
    d['bg'] = np.ascontiguousarray(inp['b_gate'][l][None, 12 * g:12 * g + 12])
    d['posT'] = np.ascontiguousarray(inp['cmp_pos'][l].transpose(2, 0, 1))
    d['w1'] = np.ascontiguousarray(inp['cmp_w1'][l].reshape(2, 32, 128, 128).transpose(0, 2, 1, 3))
    d['w2'] = np.ascontiguousarray(inp['cmp_w2'][l].transpose(1, 0, 2))
    d['gng'] = np.ascontiguousarray(inp['ret_gn_g'][l][None, 512 * g:512 * g + 512])
    d['gnb'] = np.ascontiguousarray(inp['ret_gn_b'][l][None, 512 * g:512 * g + 512])
    return {'blob': Blob(M_SHAPES).pack(d)}


def t_static(inp, l):
    order = np.concatenate([np.arange(g * 512, g * 512 + 512).tolist() + np.arange(1024 + g * 512, 1024 + g * 512 + 512).tolist()
                            for g in range(2)]).astype(int)
    d = {}
    d['wout'] = arr_tm(inp['w_mix_out'][l][order, :], 256)
    d['wq'] = arr_fm(inp['xa_wq'][l])
    d['wk'] = arr_fm(inp['xa_wk'][l])
    d['wv'] = arr_tm(inp['xa_wv'][l], 256)
    d['wo'] = arr_tm(inp['xa_wo'][l], 256)
    d['pwq'] = arr_fm(inp['peer_wq'][l])
    d['ln'] = np.stack([inp['ln1_g'][l], inp['ln1_b'][l], inp['ln2_g'][l], inp['ln2_b'][l], inp['ln3_g'][l], inp['ln3_b'][l]], 0)[:, None, :]
    d['skT'] = np.ascontiguousarray(inp['peer_sub_keys'][l].reshape(16, 128, 128).transpose(2, 0, 1))
    tab = np.concatenate([inp['peer_u'][l], inp['peer_v'][l]], 0)
    return d, tab


def t_inputs(static, tab, om_b, x_b, mem_b, hf):
    d = dict(static)
    rows = slice(hf * NT, (hf + 1) * NT)
    d['om'] = np.ascontiguousarray(om_b[rows].reshape(NT, 2048))
    d['xin'] = np.ascontiguousarray(x_b[rows])
    d['mem'] = np.ascontiguousarray(mem_b)
    return {'blob': Blob(T_SHAPES).pack(d), 'tab': tab}


def run_mixer(inp, l, x_full):
    P = build_M()
    consts = [m_consts(g) for g in range(2)]
    maps = [m_inputs(inp, l, x_full[c // 2], c % 2, consts[c % 2]) for c in range(8)]
    res = run_bass_kernel_spmd(P.nc, maps, core_ids=list(range(8)))
    om = np.stack([res.results[c]['omix'] for c in range(8)], 0)
    return om.reshape(4, 2, S, 1024).transpose(0, 2, 1, 3)


def run_token_phase(inp, l, om, x_full):
    P = build_T()
    static, tab = t_static(inp, l)
    maps = [t_inputs(static, tab, om[c // 2], x_full[c // 2], inp['mem'][c // 2], c % 2) for c in range(8)]
    res = run_bass_kernel_spmd(P.nc, maps, core_ids=list(range(8)))
    out = np.stack([res.results[c]['xout'] for c in range(8)], 0)
    return np.ascontiguousarray(out.reshape(4, S, D))


def kernel(**inputs):
    inp = {k: np.asarray(v) for k, v in inputs.items()}
    x = np.ascontiguousarray(inp['x'], dtype=np.float32)
    for l in range(DEPTH):
        om = run_mixer(inp, l, x)
        x = run_token_phase(inp, l, om, x)
    return x
```
